# Optimizing a Trainium2 kernel written in Bass

```python
import jax, jax.numpy as jnp
from jax import lax
import numpy as np

D_MODEL = 1024
BATCH = 32
SEQ = 2048
DEPTH = 2

CHUNK = 64
N_MIXERS = 2
N_SB_LAYERS = (DEPTH + 1) // 2
N_TM_LAYERS = DEPTH // 2
SB_HEADS = 16
SB_HEAD_DIM = D_MODEL // SB_HEADS
SB_QBLOCK = 128
TM_CHUNK = 128
TM_GROUPS = 8
TM_GROUP_DIM = D_MODEL // TM_GROUPS
D_FF = -(-8 * D_MODEL // (3 * 256)) * 256
EPS = 1e-6

kernel_name = "hybrid_stickbreak_tokenmlp_swiglu"


def rms_norm(x, g):
    xf = x.astype(jnp.float32)
    y = xf * lax.rsqrt(jnp.mean(xf * xf, axis=-1, keepdims=True) + EPS)
    return (y * g.astype(jnp.float32)).astype(x.dtype)


def layer_norm(x, g):
    xf = x.astype(jnp.float32)
    mu = jnp.mean(xf, axis=-1, keepdims=True)
    xc = xf - mu
    y = xc * lax.rsqrt(jnp.mean(xc * xc, axis=-1, keepdims=True) + EPS)
    return (y * g.astype(jnp.float32)).astype(x.dtype)


def ada_modulation(c, w, b):
    m = jax.nn.silu(c) @ w + b
    shift, scale, gate = jnp.split(m, 3, axis=-1)
    return shift[:, None, :], scale[:, None, :], gate[:, None, :]


def stick_breaking_attention(q, k, v):
    S = q.shape[2]
    scale = SB_HEAD_DIM ** -0.5
    outs = []
    for qb in range(S // SB_QBLOCK):
        q0 = qb * SB_QBLOCK
        kend = q0 + SB_QBLOCK
        z = jnp.einsum('bhtd,bhsd->bhts', q[:, :, q0:kend], k[:, :, :kend]).astype(jnp.float32) * scale
        t_idx = q0 + jnp.arange(SB_QBLOCK)[:, None]
        s_idx = jnp.arange(kend)[None, :]
        before = s_idx < t_idx
        log_beta = jax.nn.log_sigmoid(z)
        log_1m = jnp.where(before, jax.nn.log_sigmoid(-z), 0.0)
        later = lax.cumsum(log_1m, axis=3, reverse=True) - log_1m
        a = jnp.where(before, jnp.exp(log_beta + later), 0.0)
        outs.append(jnp.einsum('bhts,bhsd->bhtd', a.astype(v.dtype), v[:, :, :kend]))
    return jnp.concatenate(outs, axis=2)


def stick_breaking_mixer(h, w_qkv, w_o):
    B, S, D = h.shape
    qkv = (h @ w_qkv).reshape(B, S, 3, SB_HEADS, SB_HEAD_DIM)
    q = qkv[:, :, 0].transpose(0, 2, 1, 3)
    k = qkv[:, :, 1].transpose(0, 2, 1, 3)
    v = qkv[:, :, 2].transpose(0, 2, 1, 3)
    o = stick_breaking_attention(q, k, v)
    return o.transpose(0, 2, 1, 3).reshape(B, S, D) @ w_o


def token_mixing_mixer(h, w_in, ln_g, w_s, b_s, w_out):
    B, S, D = h.shape
    uv = jax.nn.gelu(h @ w_in)
    u, v = jnp.split(uv, 2, axis=-1)
    v = layer_norm(v, ln_g)
    n = S // TM_CHUNK
    v = v.reshape(B, n, TM_CHUNK, TM_GROUPS, TM_GROUP_DIM)
    pos = jnp.arange(TM_CHUNK)
    chunk_causal = (pos[:, None] // CHUNK) >= (pos[None, :] // CHUNK)
    ws = jnp.where(chunk_causal[None], w_s, 0.0).astype(v.dtype)
    sv = jnp.einsum('gts,bnsgc->bntgc', ws, v) + b_s.T[None, None, :, :, None]
    y = u * sv.reshape(B, S, D)
    return y @ w_out


def swiglu(h, w_gate, w_up, w_down):
    return (jax.nn.silu(h @ w_gate) * (h @ w_up)) @ w_down


def setup_inputs(seed: int = 0) -> dict:
    key = jax.random.key(seed)
    ks = jax.random.split(key, 24)
    D = D_MODEL

    def nrm(k, shape, s):
        return jax.random.normal(k, shape, jnp.float32) * s

    return {
        "x": nrm(ks[0], (BATCH, SEQ, D), 1.0),
        "c": nrm(ks[1], (BATCH, D), 1.0),
        "mix_ada_w": nrm(ks[2], (DEPTH, D, 3 * D), 0.5 * D ** -0.5),
        "mix_ada_b": nrm(ks[3], (DEPTH, 3 * D), 0.02),
        "mix_pre_g": 1.0 + nrm(ks[4], (DEPTH, D), 0.02),
        "mix_post_g": 1.0 + nrm(ks[5], (DEPTH, D), 0.02),
        "sb_w_qkv": nrm(ks[6], (N_SB_LAYERS, D, 3 * D), D ** -0.5),
        "sb_w_o": nrm(ks[7], (N_SB_LAYERS, D, D), D ** -0.5),
        "tm_w_in": nrm(ks[8], (N_TM_LAYERS, D, 2 * D), D ** -0.5),
        "tm_ln_g": 1.0 + nrm(ks[9], (N_TM_LAYERS, D), 0.02),
        "tm_w_s": nrm(ks[10], (N_TM_LAYERS, TM_GROUPS, TM_CHUNK, TM_CHUNK), TM_CHUNK ** -0.5),
        "tm_b_s": 1.0 + nrm(ks[11], (N_TM_LAYERS, TM_GROUPS, TM_CHUNK), 0.02),
        "tm_w_out": nrm(ks[12], (N_TM_LAYERS, D, D), D ** -0.5),
        "ffn_ada_w": nrm(ks[13], (DEPTH, D, 3 * D), 0.5 * D ** -0.5),
        "ffn_ada_b": nrm(ks[14], (DEPTH, 3 * D), 0.02),
        "ffn_pre_g": 1.0 + nrm(ks[15], (DEPTH, D), 0.02),
        "ffn_post_g": 1.0 + nrm(ks[16], (DEPTH, D), 0.02),
        "ffn_w_gate": nrm(ks[17], (DEPTH, D, D_FF), D ** -0.5),
        "ffn_w_up": nrm(ks[18], (DEPTH, D, D_FF), D ** -0.5),
        "ffn_w_down": nrm(ks[19], (DEPTH, D_FF, D), D_FF ** -0.5),
    }


def reference(x, c, mix_ada_w, mix_ada_b, mix_pre_g, mix_post_g, sb_w_qkv, sb_w_o,
              tm_w_in, tm_ln_g, tm_w_s, tm_b_s, tm_w_out, ffn_ada_w, ffn_ada_b,
              ffn_pre_g, ffn_post_g, ffn_w_gate, ffn_w_up, ffn_w_down):
    for i in range(DEPTH):
        shift, scale, gate = ada_modulation(c, mix_ada_w[i], mix_ada_b[i])
        h = rms_norm(x, mix_pre_g[i]) * (1.0 + scale) + shift
        j = i // N_MIXERS
        if i % N_MIXERS == 0:
            y = stick_breaking_mixer(h, sb_w_qkv[j], sb_w_o[j])
        else:
            y = token_mixing_mixer(h, tm_w_in[j], tm_ln_g[j], tm_w_s[j], tm_b_s[j], tm_w_out[j])
        x = x + gate * rms_norm(y, mix_post_g[i])
        shift, scale, gate = ada_modulation(c, ffn_ada_w[i], ffn_ada_b[i])
        h = rms_norm(x, ffn_pre_g[i]) * (1.0 + scale) + shift
        y = swiglu(h, ffn_w_gate[i], ffn_w_up[i], ffn_w_down[i])
        x = x + gate * rms_norm(y, ffn_post_g[i])
    return x
```

```python
import numpy as np
from contextlib import ExitStack
import concourse.bass as bass
import concourse.mybir as mybir
from concourse.bass_utils import run_bass_kernel_spmd

F32 = mybir.dt.float32
BF16 = mybir.dt.bfloat16
AF = mybir.ActivationFunctionType
ALU = mybir.AluOpType

D = 1024
KC = 8
SEQ = 2048
U = 1024
TBW = 512
DFF = 2816
FCN = 22
NH = 16
DH = 64
EPS = 1e-6
NCORES = 8
SB_BASE = 16512
SB_TOP = 229344
NEG = -30000.0


class Buf:
    __slots__ = ("name", "w", "r")

    def __init__(self, name):
        self.name = name
        self.w = None
        self.r = {}


class Sched:
    ENGS = ("pe", "act", "dve", "pool", "sp")

    def __init__(self, nc, es):
        self.nc = nc
        self.es = es
        self.streams = {e: [] for e in self.ENGS}
        self.sems = {}
        self.counts = {}
        self.seen = {e: {} for e in self.ENGS}
        for e in ("pe", "act", "dve", "pool"):
            self.new_sem("c_" + e)

    def new_sem(self, name):
        self.sems[name] = self.es.enter_context(self.nc.semaphore(name))
        self.counts[name] = 0
        return name

    def _deps(self, eng, reads, writes):
        own = "c_" + eng
        deps = {}

        def add(ev, raw):
            if ev is None:
                return
            s, v = ev
            if s == own and eng == "pe":
                return
            if deps.get(s, 0) < v:
                deps[s] = v
        for b in reads:
            add(b.w, True)
        for b in writes:
            add(b.w, False)
            for ev in b.r.values():
                add(ev, False)
        for s, v in deps.items():
            if self.seen[eng].get(s, 0) < v:
                self.seen[eng][s] = v
                self.streams[eng].append(("wait", s, v))

    def op(self, eng, fn, reads=(), writes=()):
        self._deps(eng, reads, writes)
        s = "c_" + eng
        self.counts[s] += 1
        ev = (s, self.counts[s])
        self.streams[eng].append(("op", fn, s, 1))
        for b in reads:
            b.r[s] = ev
        for b in writes:
            b.w = ev
            b.r = {}
        return ev

    def dma(self, eng, fn, sem, reads=(), writes=()):
        self._deps(eng, reads, writes)
        self.counts[sem] += 16
        ev = (sem, self.counts[sem])
        self.streams[eng].append(("op", fn, sem, 16))
        for b in reads:
            b.r[sem] = ev
        for b in writes:
            b.w = ev
            b.r = {}
        return ev

    def alias(self, old, new):
        evs = {}
        for b in old:
            for ev in [b.w] + list(b.r.values()):
                if ev is not None and evs.get(ev[0], 0) < ev[1]:
                    evs[ev[0]] = ev[1]
        for nb in new:
            for s, v in evs.items():
                if nb.r.get(s, (s, 0))[1] < v:
                    nb.r[s] = (s, v)

    def wait_all(self, eng, bufs):
        self._deps(eng, bufs, bufs)

    def emit(self):
        nc = self.nc
        sems = self.sems
        streams = self.streams
        with nc.Block() as block:
            def run(engh, lst):
                for it in lst:
                    if it[0] == "wait":
                        engh.wait_ge(sems[it[1]], it[2])
                    else:
                        it[1](engh).then_inc(sems[it[2]], it[3])

            @block.tensor
            def _(e):
                run(e, streams["pe"])

            @block.scalar
            def _(e):
                run(e, streams["act"])

            @block.vector
            def _(e):
                run(e, streams["dve"])

            @block.gpsimd
            def _(e):
                run(e, streams["pool"])

            @block.sync
            def _(e):
                run(e, streams["sp"])


def build_program(NSEQ, stop_after=4):
    nc = bass.Bass("TRN2", target_bir_lowering=False)
    NTOK = NSEQ * SEQ

    def din(name, shape, dt=F32):
        return nc.dram_tensor(name, list(shape), dt, kind="ExternalInput").ap()

    x_d = din("x", [NTOK, D])
    cT_d = din("cT", [128, KC, NSEQ])
    adaw_d = [din("mix_ada_w", [2, D, 3 * D]), din("ffn_ada_w", [2, D, 3 * D])]
    adabT_d = din("adabT", [128, 4, 24])
    preT_d = din("preT", [128, 4, KC])
    postT_d = din("postT", [128, 4, KC])
    lngB_d = din("lngB", [128, D])
    ws_d = din("tm_w_s", [8, 128, 128])
    bs_d = din("tm_b_s", [1, D])
    wqkv_d = din("sb_w_qkv", [D, 3 * D])
    wo_d = din("sb_w_o", [D, D])
    win_d = din("tm_w_in", [D, 2 * D])
    wout_d = din("tm_w_out", [D, D])
    wg_d = din("ffn_w_gate", [2, D, DFF])
    wu_d = din("ffn_w_up", [2, D, DFF])
    wd_d = din("ffn_w_down", [2, DFF, D])
    out_d = nc.dram_tensor("out", [NTOK, D], F32, kind="ExternalOutput").ap()
    ks_d = nc.dram_tensor("ks_scr", [128, KC, U], BF16, kind="Internal").ap()
    vs_d = nc.dram_tensor("vs_scr", [128, 8, D], BF16, kind="Internal").ap()

    with ExitStack() as es:
        S = Sched(nc, es)
        off = [SB_BASE]

        def dsize(dt):
            return 4 if dt == F32 else 2

        def alloc(name, shape, dt, at=None):
            n = 1
            for v in shape[1:]:
                n *= v
            nb = (n * dsize(dt) + 31) // 32 * 32
            o = off[0] if at is None else at
            assert o + nb <= SB_TOP, (name, o, nb)
            t = nc.alloc_sbuf_tensor_at(name, list(shape), dt, offset=o)
            if at is None:
                off[0] += nb
            return t

        xT = alloc("xT", [128, KC, U], F32)
        xB = [Buf("xT0"), Buf("xT1")]
        HY = off[0]
        hT = alloc("hT", [128, KC, U], BF16, at=HY)
        ysb = alloc("ysb", [128, KC, U], F32)
        hB = [Buf("h0"), Buf("h1")]
        yB = [Buf("y0"), Buf("y1")]
        NSLOT = 3
        ring = []
        for i in range(NSLOT):
            ring.append((alloc(f"ring{i}", [128, 4096], BF16), Buf(f"ring{i}"), S.new_sem(f"d_ring{i}")))
        ident = alloc("ident", [128, 128], F32); identB = Buf("ident")
        identb = alloc("identb", [128, 128], BF16); identbB = Buf("identb")
        onesb = alloc("onesb", [128, 128], BF16); onesB = Buf("onesb")
        negtri = alloc("negtri", [128, 128], BF16); negtriB = Buf("negtri")
        negones = alloc("negones", [128, 128], BF16); negonesB = Buf("negones")
        maskd = alloc("maskd", [128, 128], BF16); maskdB = Buf("maskd")
        wsT = alloc("wsT", [128, 8, 128], BF16); wsTB = Buf("wsT")
        lngB = alloc("lngB", [128, D], F32); lngBB = Buf("lngB")
        bshi = alloc("bshi", [1, D], BF16); bshiB = Buf("bshi")
        bslo = alloc("bslo", [1, D], BF16); bsloB = Buf("bslo")
        modA = alloc("modA", [128, 4, KC, NSEQ], F32)
        modB = alloc("modB", [128, 4, KC, NSEQ], F32)
        modG = alloc("modG", [128, 4, KC, NSEQ], F32)
        modBuf = Buf("mod")
        mraw = alloc("mraw", [128, 24, NSEQ], F32); mrawB = Buf("mraw")
        adabT = alloc("adabT", [128, 4, 24], F32)
        preT = alloc("preT", [128, 4, KC], F32)
        postT = alloc("postT", [128, 4, KC], F32)
        cT = alloc("cT", [128, KC, NSEQ], F32)
        scT = alloc("scT", [128, KC, NSEQ], BF16); scTB = Buf("scT")
        smallB = Buf("small_in")
        NSQ = 3
        sq = [(alloc(f"sq{i}", [128, TBW], BF16), Buf(f"sq{i}")) for i in range(NSQ)]
        rs = [(alloc(f"rs{i}", [128, TBW], F32), Buf(f"rs{i}")) for i in range(2)]
        tmp = [(alloc(f"tmp{i}", [128, TBW], F32), Buf(f"tmp{i}")) for i in range(2)]
        lnst = alloc("lnst", [128, 8, 12], F32); lnstB = Buf("lnst")
        lnmv = alloc("lnmv", [128, 8, 2], F32); lnmvB = Buf("lnmv")
        lnrs = alloc("lnrs", [128, 8], F32); lnrsB = Buf("lnrs")
        PH = off[0]
        PH_BYTES = SB_TOP - PH
        o = PH
        kT = alloc("kT", [128, KC, SEQ], BF16, at=o); o += KC * SEQ * 2
        Vt = alloc("Vt", [128, 16, D], BF16, at=o); o += 16 * D * 2
        qT = alloc("qT", [128, KC, U], BF16, at=o); o += KC * U * 2
        ebuf = tmp
        spbuf = []
        for i in range(3):
            spbuf.append((alloc(f"spbuf{i}", [128, TBW], BF16, at=o), Buf(f"spbuf{i}"))); o += TBW * 2
        abuf = []
        for i in range(3):
            abuf.append((alloc(f"abuf{i}", [128, TBW], BF16, at=o), Buf(f"abuf{i}"))); o += TBW * 2
        ssuf = []
        for i in range(2):
            ssuf.append((alloc(f"ssuf{i}", [128, TBW], BF16, at=o), Buf(f"ssuf{i}"))); o += TBW * 2
        assert o <= SB_TOP, o
        kB = [Buf("k0"), Buf("k1")]
        vB = [Buf("v0"), Buf("v1")]
        qB = [[Buf(f"q{h}_{c}") for c in range(2)] for h in range(NH)]
        att_bufs = kB + vB + [b for l in qB for b in l] + [b for _, b in spbuf + abuf + ssuf]
        aT = alloc("aT", [128, FCN, U], BF16, at=PH)
        aB = [Buf("a0"), Buf("a1")]
        ffn_bufs = aB
        o = PH
        uT = alloc("uT", [128, KC, U], BF16, at=o); o += KC * U * 2
        vg = alloc("vg", [128, 8, D], F32, at=o); o += 8 * D * 4
        vn = alloc("vn", [128, 8, D], BF16, at=o); o += 8 * D * 2
        uB = [Buf("u0"), Buf("u1")]
        vgB = [Buf(f"vg{t}") for t in range(8)]
        vnB = [Buf(f"vn{t}") for t in range(8)]
        tm_bufs = uB + vgB + vnB
        stg = []
        o = PH
        for i in range(2):
            stg.append((alloc(f"stg{i}", [128, D], F32, at=o), Buf(f"stg{i}"), S.new_sem(f"d_stg{i}"))); o += D * 4
        stg_bufs = [b for _, b, _ in stg]
        bsrow = alloc("bsrow", [1, D], F32, at=o); bsrowB = Buf("bsrow"); o += D * 4
        bstmp = alloc("bstmp", [1, D], F32, at=o); bstmpB = Buf("bstmp"); o += D * 4
        setup_bufs = [bsrowB, bstmpB]
        pb = [es.enter_context(nc.psum_tensor(f"pb{i}", [128, TBW], F32)) for i in range(8)]
        pB = [Buf(f"pb{i}") for i in range(8)]

        def MM(out, lhsT, rhs, start, stop, R, W, skip=False):
            S.op("pe", lambda e: e.matmul(out, lhsT=lhsT, rhs=rhs, start=start, stop=stop, skip_group_check=skip), R, W)

        def TR(out, in_, R, W):
            S.op("pe", lambda e: e.transpose(out=out, in_=in_, identity=ident[:]), list(R) + [identB], W)

        def ACT(out, in_, func, R, W, bias=None, scale=None):
            kw = {}
            if bias is not None:
                kw["bias"] = bias
            if scale is not None:
                kw["scale"] = scale
            S.op("act", lambda e: e.activation(out=out, in_=in_, func=func, **kw), R, W)

        def TT(eng, out, in0, in1, op, R, W):
            S.op(eng, lambda e: e.tensor_tensor(out=out, in0=in0, in1=in1, op=op), R, W)

        def TS(eng, out, in0, s1, s2, op0, op1, R, W):
            if op1 is None:
                S.op(eng, lambda e: e.tensor_scalar(out=out, in0=in0, scalar1=s1, scalar2=None, op0=op0), R, W)
            else:
                S.op(eng, lambda e: e.tensor_scalar(out=out, in0=in0, scalar1=s1, scalar2=s2, op0=op0, op1=op1), R, W)

        def STT(eng, out, in0, scalar, in1, op0, op1, R, W):
            S.op(eng, lambda e: e.scalar_tensor_tensor(out=out, in0=in0, scalar=scalar, in1=in1, op0=op0, op1=op1), R, W)

        def CP(eng, out, in_, R, W):
            if eng == "act":
                ACT(out, in_, AF.Copy, R, W)
            else:
                S.op(eng, lambda e: e.tensor_copy(out=out, in_=in_), R, W)

        def MEMSET(eng, ap, val, W):
            S.op(eng, lambda e: e.memset(ap, val), (), W)

        ring_i = [0]

        def load_w(src_ap, k, n):
            t, b, sem = ring[ring_i[0] % NSLOT]
            ring_i[0] += 1
            view = t[:, 0:k * n].rearrange("p (k n) -> p k n", k=k)
            S.dma("pool", lambda e: e.dma_start(out=view, in_=src_ap), sem, (), [b])
            return view, b

        class Rot:
            def __init__(self, ids):
                self.ids = list(ids)
                self.i = 0

            def next(self):
                v = self.ids[self.i % len(self.ids)]
                self.i += 1
                return v

        evac_i = [0]

        def evac_eng():
            evac_i[0] += 1
            return "dve" if evac_i[0] % 2 else "act"

        S.new_sem("d_setup")
        setup_loads = [(adabT, adabT_d), (preT, preT_d), (postT, postT_d), (cT, cT_d), (lngB, lngB_d), (bsrow, bs_d)]
        for t, d in setup_loads:
            S.dma("sp", (lambda e, t=t, d=d: e.dma_start(out=t[:], in_=d)), "d_setup", (), [])
        tot = S.counts["d_setup"]
        for b in (smallB, lngBB, bsrowB):
            b.w = ("d_setup", tot)
        MEMSET("dve", ident[:], 0.0, [identB])
        S.op("pool", lambda e: e.affine_select(out=ident[:], in_=ident[:], pattern=[[-1, 128]], compare_op=ALU.not_equal,
                                               fill=1.0, base=0, channel_multiplier=1), [identB], [identB])
        CP("dve", identb[:], ident[:], [identB], [identbB])
        MEMSET("dve", onesb[:], 1.0, [onesB])
        MEMSET("dve", negones[:], -1.0, [negonesB])
        t0, t0B = tmp[0]
        t1, t1B = tmp[1]
        MEMSET("dve", t0[:, 0:128], -1.0, [t0B])
        S.op("pool", lambda e: e.affine_select(out=t0[:, 0:128], in_=t0[:, 0:128], pattern=[[-1, 128]], compare_op=ALU.is_ge,
                                               fill=0.0, base=0, channel_multiplier=1), [t0B], [t0B])
        CP("dve", negtri[:], t0[:, 0:128], [t0B], [negtriB])
        TS("dve", maskd[:], t0[:, 0:128], -NEG, None, ALU.mult, None, [t0B], [maskdB])
        MEMSET("dve", t1[:, 0:128], 1.0, [t1B])
        MEMSET("dve", t1[64:128, 0:64], 0.0, [t1B])
        tb_rot = Rot(range(8))
        for g in range(8):
            st, sB, ssem = stg[g % 2]
            S.dma("sp", (lambda e, st=st, g=g: e.dma_start(out=st[:, 0:128], in_=ws_d[g])), ssem, (), [sB])
            bi = tb_rot.next()
            TR(pb[bi][:, 0:128], st[:, 0:128], [sB], [pB[bi]])
            TT("dve", wsT[:, g, :], pb[bi][:, 0:128], t1[:, 0:128], ALU.mult, [pB[bi], t1B], [wsTB])
        CP("dve", bshi[:], bsrow[:], [bsrowB], [bshiB])
        CP("dve", bstmp[:], bshi[:], [bshiB], [bstmpB])
        TT("dve", bstmp[:], bsrow[:], bstmp[:], ALU.subtract, [bsrowB, bstmpB], [bstmpB])
        CP("dve", bslo[:], bstmp[:], [bstmpB], [bsloB])
        ACT(scT[:], cT[:], AF.Silu, [smallB], [scTB])
        for sl in range(4):
            wsrc = adaw_d[sl % 2][sl // 2]
            mb = 7
            for fb in range(6):
                view, rb = load_w(wsrc[:, fb * 512:(fb + 1) * 512].rearrange("(k p) f -> p k f", p=128), KC, 512)
                for fc in range(4):
                    j = fb * 4 + fc
                    for k in range(KC):
                        MM(pb[mb][:, j * NSEQ:(j + 1) * NSEQ], view[:, k, fc * 128:(fc + 1) * 128], scT[:, k, :],
                           k == 0, k == KC - 1, [rb, scTB], [pB[mb]])
            mview = pb[mb][:, 0:24 * NSEQ].rearrange("p (j b) -> p j b", b=NSEQ)
            for b in range(NSEQ):
                TT("dve", mraw[:, :, b], mview[:, :, b], adabT[:, sl, :], ALU.add, [pB[mb], smallB], [mrawB])
                STT("dve", modA[:, sl, :, b], mraw[:, 8:16, b], 1.0, preT[:, sl, :], ALU.add, ALU.mult, [mrawB, smallB], [modBuf])
                CP("dve", modB[:, sl, :, b], mraw[:, 0:8, b], [mrawB], [modBuf])
                TT("dve", modG[:, sl, :, b], mraw[:, 16:24, b], postT[:, sl, :], ALU.mult, [mrawB, smallB], [modBuf])

        def tbs(tb):
            return slice(tb * TBW, (tb + 1) * TBW)

        sq_i = [0]

        def prenorm(sl, b):
            for tb in range(2):
                sb_ = 6 + tb
                for k in range(KC):
                    sqt, sqB = sq[sq_i[0] % NSQ]; sq_i[0] += 1
                    TT("dve", sqt[:], xT[:, k, tbs(tb)], xT[:, k, tbs(tb)], ALU.mult, [xB[tb]], [sqB])
                    MM(pb[sb_][:], onesb[:], sqt[:], k == 0, k == KC - 1, [sqB, onesB], [pB[sb_]])
                rst, rsB = rs[tb]
                ACT(rst[:], pb[sb_][:], AF.Ln, [pB[sb_]], [rsB], bias=EPS, scale=1.0 / D)
                ACT(rst[:], rst[:], AF.Exp, [rsB], [rsB], scale=-0.5)
                for k in range(KC):
                    tt_, ttB = tmp[k % 2]
                    STT("dve", tt_[:], xT[:, k, tbs(tb)], modA[:, sl, k, b:b + 1], rst[:], ALU.mult, ALU.mult,
                        [xB[tb], rsB, modBuf], [ttB])
                    ACT(hT[:, k, tbs(tb)], tt_[:], AF.Identity, [ttB, modBuf], [hB[tb]], bias=modB[:, sl, k, b:b + 1], scale=1.0)

        def proj_B(wsrc, f0, nf, src, srcB, evac, rot):
            fcg = 0
            f = f0
            while f < f0 + nf:
                w = min(512, f0 + nf - f)
                view, rb = load_w(wsrc[:, f:f + w].rearrange("(k p) f -> p k f", p=128), KC, w)
                for fc in range(w // 128):
                    bks = [rot.next(), rot.next()]
                    for k in range(KC):
                        for tb in range(2):
                            MM(pb[bks[tb]][:], view[:, k, fc * 128:(fc + 1) * 128], src[:, k, tbs(tb)],
                               k == 0, k == KC - 1, [rb, srcB[tb]], [pB[bks[tb]]])
                    for tb in range(2):
                        evac(fcg, tb, bks[tb])
                    fcg += 1
                f += w

        def proj_A(wsrc, f0, nf, src, srcB, evac, rot):
            for fb in range(nf // 512):
                view, rb = load_w(wsrc[:, f0 + fb * 512:f0 + (fb + 1) * 512].rearrange("(k p) f -> p k f", p=128), KC, 512)
                for tt in range(8):
                    bk = rot.next()
                    for k in range(KC):
                        MM(pb[bk][:], src[:, k, tt * 128:(tt + 1) * 128], view[:, k, :], k == 0, k == KC - 1,
                           [rb, srcB[tt // 4]], [pB[bk]])
                    evac(fb, tt, bk)

        def post_collect(dc, tb, bk):
            sqt, sqB = sq[sq_i[0] % NSQ]; sq_i[0] += 1
            CP("act", ysb[:, dc, tbs(tb)], pb[bk][:], [pB[bk]], [yB[tb]])
            TT("dve", sqt[:], ysb[:, dc, tbs(tb)], ysb[:, dc, tbs(tb)], ALU.mult, [yB[tb]], [sqB])
            MM(pb[6 + tb][:], onesb[:], sqt[:], dc == 0, dc == KC - 1, [sqB, onesB], [pB[6 + tb]])

        def post_finish(sl, b):
            for tb in range(2):
                rst, rsB = rs[tb]
                ACT(rst[:], pb[6 + tb][:], AF.Ln, [pB[6 + tb]], [rsB], bias=EPS, scale=1.0 / D)
                ACT(rst[:], rst[:], AF.Exp, [rsB], [rsB], scale=-0.5)
            for tb in range(2):
                rst, rsB = rs[tb]
                for k in range(KC):
                    tt_, ttB = tmp[k % 2]
                    TT("dve", tt_[:], ysb[:, k, tbs(tb)], rst[:], ALU.mult, [yB[tb], rsB], [ttB])
                    STT("dve", xT[:, k, tbs(tb)], tt_[:], modG[:, sl, k, b:b + 1], xT[:, k, tbs(tb)], ALU.mult, ALU.add,
                        [ttB, xB[tb], modBuf], [xB[tb]])

        def load_unit(row0):
            for t in range(8):
                st, sB, ssem = stg[t % 2]
                S.dma("sp", (lambda e, st=st, r=row0 + t * 128: e.dma_start(out=st[:], in_=x_d[r:r + 128, :])), ssem, (), [sB])
                for half in range(2):
                    bk = (2 * t + half) % 6
                    for j in range(4):
                        kk = half * 4 + j
                        TR(pb[bk][:, j * 128:(j + 1) * 128], st[:, kk * 128:(kk + 1) * 128], [sB], [pB[bk]])
                    eng = "dve" if half == 0 else "act"
                    CP(eng, xT[:, half * 4:(half + 1) * 4, t * 128:(t + 1) * 128],
                       pb[bk][:].rearrange("p (k n) -> p k n", k=4), [pB[bk]], [xB[t // 4]])

        def store_unit(row0):
            for t in range(8):
                st, sB, ssem = stg[t % 2]
                for half in range(2):
                    bk = (2 * t + half) % 6
                    for j in range(4):
                        kk = half * 4 + j
                        TR(pb[bk][:, j * 128:(j + 1) * 128], xT[:, kk, t * 128:(t + 1) * 128], [xB[t // 4]], [pB[bk]])
                    eng = "dve" if half == 0 else "act"
                    CP(eng, st[:, half * 512:(half + 1) * 512], pb[bk][:], [pB[bk]], [sB])
                S.dma("sp", (lambda e, st=st, r=row0 + t * 128: e.dma_start(out=out_d[r:r + 128, :], in_=st[:])), ssem, [sB], ())

        def attention_mixer(b, u):
            sl = 0
            if stop_after < 0.3:
                return
            prenorm(sl, b)
            S.alias(stg_bufs + ffn_bufs + tm_bufs + setup_bufs, att_bufs)
            if stop_after < 0.5:
                return
            rot = Rot(range(6))
            if u == 1:
                S.dma("sp", lambda e: e.dma_start(out=kT[:, :, 0:U], in_=ks_d), "d_kv", (), [kB[0]])
                S.dma("sp", lambda e: e.dma_start(out=Vt[:, 0:8, :], in_=vs_d), "d_kv2", (), [vB[0]])

            def evac_q(fc, tb, bk):
                eng = evac_eng()
                if eng == "act":
                    ACT(qT[:, fc, tbs(tb)], pb[bk][:], AF.Copy, [pB[bk]], [qB[2 * fc][tb], qB[2 * fc + 1][tb]], scale=DH ** -0.5)
                else:
                    TS("dve", qT[:, fc, tbs(tb)], pb[bk][:], DH ** -0.5, None, ALU.mult, None, [pB[bk]],
                       [qB[2 * fc][tb], qB[2 * fc + 1][tb]])

            def evac_k(fc, tb, bk):
                CP(evac_eng(), kT[:, fc, u * U + tb * TBW:u * U + (tb + 1) * TBW], pb[bk][:], [pB[bk]], [kB[u]])

            def evac_v(fb, tt, bk):
                CP(evac_eng(), Vt[:, u * 8 + tt, fb * 512:(fb + 1) * 512], pb[bk][:], [pB[bk]], [vB[u]])

            proj_B(wqkv_d, 0, D, hT, hB, evac_q, rot)
            proj_B(wqkv_d, D, D, hT, hB, evac_k, rot)
            proj_A(wqkv_d, 2 * D, D, hT, hB, evac_v, rot)
            if u == 0:
                S.dma("sp", lambda e: e.dma_start(out=ks_d, in_=kT[:, :, 0:U]), "d_kv", [kB[0]], ())
                S.dma("sp", lambda e: e.dma_start(out=vs_d, in_=Vt[:, 0:8, :]), "d_kv2", [vB[0]], ())

            if stop_after < 0.7:
                return
            items = []
            for h in range(NH):
                for cl in range(2):
                    cg = 2 * u + cl
                    kbs = list(range(4 * cg + 3, -1, -1))
                    for i, kb in enumerate(kbs):
                        items.append((h, cl, cg, kb, i == 0, i == len(kbs) - 1))
            n = len(items)
            zrot = [0, 1, 2, 3]
            orot = [4, 5]
            st8 = {}

            def info(t):
                h, cl, cg, kb, first, last = items[t]
                sid = h * 2 + cl
                j = kb - 4 * cg
                c0 = 128 * j if j >= 0 else 0
                return h, cl, cg, kb, first, last, sid, j, c0

            def S1(t):
                h, cl, cg, kb, first, last, sid, j, c0 = info(t)
                zb = zrot[t % 4]
                po = (h % 2) * 64
                ch = h // 2
                kbuf = kB[kb // 8]
                MM(pb[zb][:, c0:TBW], kT[po:po + 64, ch, kb * 128:(kb + 1) * 128], qT[po:po + 64, ch, cl * TBW + c0:(cl + 1) * TBW],
                   True, j < 0, [kbuf, qB[h][cl]], [pB[zb]])
                if j >= 0:
                    MM(pb[zb][:, c0:c0 + 128], identb[:], maskd[:], False, True, [identbB, maskdB], [pB[zb]])

            def S2(t):
                h, cl, cg, kb, first, last, sid, j, c0 = info(t)
                zb = zrot[t % 4]
                et, eB = ebuf[t % 2]
                spt, spB = spbuf[t % 3]
                ACT(et[:, c0:TBW], pb[zb][:, c0:TBW], AF.Exp, [pB[zb]], [eB])
                ACT(spt[:, c0:TBW], et[:, c0:TBW], AF.Ln, [eB], [spB], bias=1.0, scale=1.0)

            def S3(t):
                h, cl, cg, kb, first, last, sid, j, c0 = info(t)
                zb = zrot[t % 4]
                spt, spB = spbuf[t % 3]
                sst, ssB = ssuf[sid % 2]
                MM(pb[zb][:, c0:TBW], negtri[:], spt[:, c0:TBW], False, first, [negtriB, spB], [pB[zb]], skip=True)
                if first:
                    MEMSET("dve", sst[:], 0.0, [ssB])
                else:
                    MM(pb[zb][:, c0:TBW], negones[:], sst[:, c0:TBW], False, True, [negonesB, ssB], [pB[zb]], skip=True)
                if not last:
                    TT("dve", sst[:, c0:TBW], sst[:, c0:TBW], spt[:, c0:TBW], ALU.add, [ssB, spB], [ssB])

            def S4(t):
                h, cl, cg, kb, first, last, sid, j, c0 = info(t)
                zb = zrot[t % 4]
                at, aB_ = abuf[t % 3]
                ACT(at[:, c0:TBW], pb[zb][:, c0:TBW], AF.Exp, [pB[zb]], [aB_])

            def S5(t):
                h, cl, cg, kb, first, last, sid, j, c0 = info(t)
                at, aB_ = abuf[t % 3]
                ob = orot[sid % 2]
                po = (h % 2) * 64
                vbuf = vB[kb // 8]
                lhs = Vt[:, kb, h * DH:(h + 1) * DH]
                if j >= 0:
                    for jj in range(j, 4):
                        MM(pb[ob][po:po + 64, jj * 128:(jj + 1) * 128], lhs, at[:, jj * 128:(jj + 1) * 128],
                           first, last and jj == 3, [vbuf, aB_], [pB[ob]])
                else:
                    MM(pb[ob][po:po + 64, :], lhs, at[:], False, last, [vbuf, aB_], [pB[ob]])
                if last:
                    CP("dve", qT[po:po + 64, h // 2, cl * TBW:(cl + 1) * TBW], pb[ob][po:po + 64, :], [pB[ob]], [qB[h][cl]])

            for t in range(n + 3):
                if t < n:
                    S1(t)
                if 0 <= t - 1 < n:
                    S2(t - 1)
                if 0 <= t - 2 < n:
                    S3(t - 2)
                    S4(t - 2)
                if 0 <= t - 3 < n:
                    S5(t - 3)

            if stop_after < 1:
                return
            S.alias(hB, yB)
            oB = [[qB[h][tb] for h in range(NH)] for tb in range(2)]

            proj_B_multi(wo_d, 0, D, qT, oB, post_collect, Rot(range(6)))
            post_finish(sl, b)
            S.alias(yB, hB)

        def proj_B_multi(wsrc, f0, nf, src, srcBl, evac, rot):
            fcg = 0
            f = f0
            while f < f0 + nf:
                w = min(512, f0 + nf - f)
                view, rb = load_w(wsrc[:, f:f + w].rearrange("(k p) f -> p k f", p=128), KC, w)
                for fc in range(w // 128):
                    bks = [rot.next(), rot.next()]
                    for k in range(KC):
                        for tb in range(2):
                            MM(pb[bks[tb]][:], view[:, k, fc * 128:(fc + 1) * 128], src[:, k, tbs(tb)],
                               k == 0, k == KC - 1, [rb] + list(srcBl[tb]), [pB[bks[tb]]])
                    for tb in range(2):
                        evac(fcg, tb, bks[tb])
                    fcg += 1
                f += w

        def ffn(layer, b, prev_bufs):
            sl = 2 * layer + 1
            prenorm(sl, b)
            S.alias(prev_bufs, ffn_bufs)
            rot = Rot(range(8))
            f = 0
            fcg = 0
            while f < DFF:
                w = min(512, DFF - f)
                gv, gb_ = load_w(wg_d[layer, :, f:f + w].rearrange("(k p) f -> p k f", p=128), KC, w)
                uv, ub_ = load_w(wu_d[layer, :, f:f + w].rearrange("(k p) f -> p k f", p=128), KC, w)
                for fc in range(w // 128):
                    gbk = [rot.next(), rot.next()]
                    ubk = [rot.next(), rot.next()]
                    for k in range(KC):
                        for tb in range(2):
                            MM(pb[gbk[tb]][:], gv[:, k, fc * 128:(fc + 1) * 128], hT[:, k, tbs(tb)], k == 0, k == KC - 1,
                               [gb_, hB[tb]], [pB[gbk[tb]]])
                    for k in range(KC):
                        for tb in range(2):
                            MM(pb[ubk[tb]][:], uv[:, k, fc * 128:(fc + 1) * 128], hT[:, k, tbs(tb)], k == 0, k == KC - 1,
                               [ub_, hB[tb]], [pB[ubk[tb]]])
                    for tb in range(2):
                        tt_, ttB = tmp[tb]
                        ACT(tt_[:], pb[gbk[tb]][:], AF.Silu, [pB[gbk[tb]]], [ttB])
                        TT("dve", aT[:, fcg, tbs(tb)], pb[ubk[tb]][:], tt_[:], ALU.mult, [pB[ubk[tb]], ttB], [aB[tb]])
                    fcg += 1
                f += w
            S.alias(hB, yB)
            rot = Rot(range(6))
            for dg in range(4):
                slots = []
                for kh in range(2):
                    slots.append(load_w(wd_d[layer, kh * 1408:(kh + 1) * 1408, dg * 256:(dg + 1) * 256]
                                        .rearrange("(k p) f -> p k f", p=128), 11, 256))
                for dcl in range(2):
                    dc = dg * 2 + dcl
                    bks = [rot.next(), rot.next()]
                    for kk in range(FCN):
                        view, rb = slots[kk // 11]
                        for tb in range(2):
                            MM(pb[bks[tb]][:], view[:, kk % 11, dcl * 128:(dcl + 1) * 128], aT[:, kk, tbs(tb)],
                               kk == 0, kk == FCN - 1, [rb, aB[tb]], [pB[bks[tb]]])
                    for tb in range(2):
                        post_collect(dc, tb, bks[tb])
            post_finish(sl, b)
            S.alias(yB, hB)

        def tm_mixer(b):
            sl = 2
            prenorm(sl, b)
            S.alias(ffn_bufs, tm_bufs)
            rot = Rot(range(6))

            def evac_u(fc, tb, bk):
                ACT(uT[:, fc, tbs(tb)], pb[bk][:], AF.Gelu_apprx_tanh, [pB[bk]], [uB[tb]])

            def evac_vg(fb, tt, bk):
                ACT(vg[:, tt, fb * 512:(fb + 1) * 512], pb[bk][:], AF.Gelu_apprx_tanh, [pB[bk]], [vgB[tt]])

            proj_B(win_d, 0, D, hT, hB, evac_u, rot)
            proj_A(win_d, D, D, hT, hB, evac_vg, rot)
            for tt in range(8):
                for hf in range(2):
                    S.op("dve", (lambda e, tt=tt, hf=hf: e.bn_stats(out=lnst[:, tt, hf * 6:(hf + 1) * 6], in_=vg[:, tt, hf * 512:(hf + 1) * 512])),
                         [vgB[tt]], [lnstB])
                S.op("dve", (lambda e, tt=tt: e.bn_aggr(out=lnmv[:, tt, :], in_=lnst[:, tt, :])), [lnstB], [lnmvB])
            ACT(lnrs[:], lnmv[:, :, 1], AF.Ln, [lnmvB], [lnrsB], bias=EPS, scale=1.0)
            ACT(lnrs[:], lnrs[:], AF.Exp, [lnrsB], [lnrsB], scale=-0.5)
            for tt in range(8):
                TS("dve", vg[:, tt, :], vg[:, tt, :], lnmv[:, tt, 0:1], lnrs[:, tt:tt + 1], ALU.subtract, ALU.mult,
                   [vgB[tt], lnmvB, lnrsB], [vgB[tt]])
                TT("dve", vn[:, tt, :], vg[:, tt, :], lngB[:], ALU.mult, [vgB[tt], lngBB], [vnB[tt]])
            for g in range(8):
                for tb in range(2):
                    bk = rot.next()
                    for c4 in range(4):
                        cn = tb * 4 + c4
                        osl = pb[bk][:, c4 * 128:(c4 + 1) * 128]
                        MM(osl, vn[:, cn, g * 128:(g + 1) * 128], wsT[:, g, :], True, False, [vnB[cn], wsTB], [pB[bk]])
                        MM(osl, onesb[0:1, :], bshi[0:1, g * 128:(g + 1) * 128], False, False, [onesB, bshiB], [pB[bk]])
                        MM(osl, onesb[0:1, :], bslo[0:1, g * 128:(g + 1) * 128], False, True, [onesB, bsloB], [pB[bk]])
                    TT("dve", uT[:, g, tbs(tb)], pb[bk][:], uT[:, g, tbs(tb)], ALU.mult, [pB[bk], uB[tb]], [uB[tb]])
            S.alias(hB, yB)
            proj_B(wout_d, 0, D, uT, uB, post_collect, Rot(range(6)))
            post_finish(sl, b)
            S.alias(yB, hB)

        S.new_sem("d_kv")
        S.new_sem("d_kv2")
        first = True
        for sidx in range(NSEQ):
            for u in range(2):
                row0 = sidx * SEQ + u * U
                S.alias(ffn_bufs + tm_bufs + att_bufs, stg_bufs)
                load_unit(row0)
                attention_mixer(sidx, u)
                last_bufs = att_bufs
                if stop_after >= 2:
                    ffn(0, sidx, att_bufs)
                    last_bufs = ffn_bufs
                if stop_after >= 3:
                    tm_mixer(sidx)
                    last_bufs = tm_bufs
                if stop_after >= 4:
                    ffn(1, sidx, tm_bufs)
                    last_bufs = ffn_bufs
                S.alias(last_bufs, stg_bufs)
                store_unit(row0)
        S.wait_all("sp", stg_bufs)
        S.emit()
    return nc


_CACHE = {}


def _layout_inputs(inputs, NSEQ, core):
    f32 = np.float32
    b0 = core * NSEQ
    x = np.ascontiguousarray(inputs["x"][b0:b0 + NSEQ].reshape(NSEQ * SEQ, D), dtype=f32)
    c = np.asarray(inputs["c"][b0:b0 + NSEQ], dtype=f32)
    cT = np.ascontiguousarray(c.reshape(NSEQ, KC, 128).transpose(2, 1, 0))

    def fm(v, nch):
        v = np.asarray(v, dtype=f32)
        return v.reshape(v.shape[:-1] + (nch, 128))

    adab = np.stack([inputs["mix_ada_b"][0], inputs["ffn_ada_b"][0], inputs["mix_ada_b"][1], inputs["ffn_ada_b"][1]], 0)
    adabT = np.ascontiguousarray(fm(adab, 24).transpose(2, 0, 1))
    pre = np.stack([inputs["mix_pre_g"][0], inputs["ffn_pre_g"][0], inputs["mix_pre_g"][1], inputs["ffn_pre_g"][1]], 0)
    post = np.stack([inputs["mix_post_g"][0], inputs["ffn_post_g"][0], inputs["mix_post_g"][1], inputs["ffn_post_g"][1]], 0)
    preT = np.ascontiguousarray(fm(pre, KC).transpose(2, 0, 1))
    postT = np.ascontiguousarray(fm(post, KC).transpose(2, 0, 1))
    lngB = np.ascontiguousarray(np.broadcast_to(np.asarray(inputs["tm_ln_g"][0], dtype=f32)[None, :], (128, D)))
    m = {
        "x": x, "cT": cT,
        "mix_ada_w": np.asarray(inputs["mix_ada_w"], dtype=f32), "ffn_ada_w": np.asarray(inputs["ffn_ada_w"], dtype=f32),
        "adabT": adabT, "preT": preT, "postT": postT, "lngB": lngB,
        "tm_w_s": np.ascontiguousarray(np.asarray(inputs["tm_w_s"][0], dtype=f32)),
        "tm_b_s": np.ascontiguousarray(np.asarray(inputs["tm_b_s"][0], dtype=f32).reshape(1, D)),
        "sb_w_qkv": np.ascontiguousarray(np.asarray(inputs["sb_w_qkv"][0], dtype=f32)),
        "sb_w_o": np.ascontiguousarray(np.asarray(inputs["sb_w_o"][0], dtype=f32)),
        "tm_w_in": np.ascontiguousarray(np.asarray(inputs["tm_w_in"][0], dtype=f32)),
        "tm_w_out": np.ascontiguousarray(np.asarray(inputs["tm_w_out"][0], dtype=f32)),
        "ffn_w_gate": np.asarray(inputs["ffn_w_gate"], dtype=f32),
        "ffn_w_up": np.asarray(inputs["ffn_w_up"], dtype=f32),
        "ffn_w_down": np.asarray(inputs["ffn_w_down"], dtype=f32),
    }
    return m


def run(inputs, NSEQ, ncores, stop_after=4):
    key = (NSEQ, stop_after)
    if key not in _CACHE:
        _CACHE[key] = build_program(NSEQ, stop_after)
    nc = _CACHE[key]
    in_maps = [_layout_inputs(inputs, NSEQ, c) for c in range(ncores)]
    res = run_bass_kernel_spmd(nc, in_maps, core_ids=list(range(ncores)))
    outs = [np.asarray(r["out"]).reshape(NSEQ, SEQ, D) for r in res.results]
    return np.concatenate(outs, axis=0)


def kernel(**inputs):
    B = inputs["x"].shape[0]
    NSEQ = B // NCORES
    return run(inputs, NSEQ, NCORES).astype(np.float32)
```
